# Optimizing a Trainium2 kernel written in Bass

```python
import jax, jax.numpy as jnp
from jax import lax
import numpy as np

D_MODEL = 4096
BATCH = 8
SEQ = 2048
DEPTH = 1
DEC_BATCH = 2
DEC_SEQ = 4096
PAST_LEN = 128

HEAD_DIM = 128
N_Q_HEADS = 16
N_KV_HEADS = 4
Q_GROUP = N_Q_HEADS // N_KV_HEADS
Q_WIDTH = N_Q_HEADS * HEAD_DIM
KV_WIDTH = N_KV_HEADS * HEAD_DIM
Q_BLOCK = 128
ROPE_THETA = 10000.0
GRID_W = 64
POOL_WINDOWS = (2, 4, 8, 16)
N_POOL_GROUPS = len(POOL_WINDOWS)
POOL_WIDTH = D_MODEL // 2
POOL_GROUP_C = POOL_WIDTH // N_POOL_GROUPS
POOL_OUT_C = D_MODEL // N_POOL_GROUPS
N_BRANCHES = 2
IN_WIDTH = Q_WIDTH + 2 * KV_WIDTH + POOL_WIDTH + N_BRANCHES * D_MODEL
D_FF = 11008
CONV_W = 3
EPS = 1e-6

kernel_name = "hybrid_gqa_pool_convglu_encoder"


def rmsnorm(x, g):
    xf = x.astype(jnp.float32)
    inv = lax.rsqrt(jnp.mean(xf * xf, axis=-1, keepdims=True) + EPS)
    return (xf * inv * g.astype(jnp.float32)).astype(x.dtype)


def axial_rope(x, s):
    rows = s // GRID_W
    half = HEAD_DIM // 2
    quarter = half // 2
    inv_freq = ROPE_THETA ** (-jnp.arange(0, half, 2, dtype=jnp.float32) / half)
    row = jnp.repeat(jnp.arange(rows, dtype=jnp.float32), GRID_W)
    col = jnp.tile(jnp.arange(GRID_W, dtype=jnp.float32), rows)
    ang_r = row[:, None] * inv_freq[None, :]
    ang_c = col[:, None] * inv_freq[None, :]
    ang = jnp.concatenate([ang_r, ang_r, ang_c, ang_c], axis=-1)
    cos = jnp.cos(ang)[None, :, None, :]
    sin = jnp.sin(ang)[None, :, None, :]
    xf = x.astype(jnp.float32)
    xr = xf.reshape(xf.shape[:-1] + (2, 2, quarter))
    rot = jnp.stack([-xr[..., 1, :], xr[..., 0, :]], axis=-2).reshape(xf.shape)
    return (xf * cos + rot * sin).astype(x.dtype)


def blocked_gqa(q, k, v):
    b, s = q.shape[0], q.shape[1]
    nb = s // Q_BLOCK
    scale = HEAD_DIM ** -0.5
    qb = q.reshape(b, nb, Q_BLOCK, N_KV_HEADS, Q_GROUP, HEAD_DIM).transpose(1, 0, 2, 3, 4, 5)

    def one_block(q_blk):
        sc = jnp.einsum('bqkgd,bskd->bkgqs', q_blk, k).astype(jnp.float32) * scale
        p = jax.nn.softmax(sc, axis=-1).astype(v.dtype)
        return jnp.einsum('bkgqs,bskd->bqkgd', p, v)

    o = lax.map(one_block, qb)
    return o.transpose(1, 0, 2, 3, 4, 5).reshape(b, s, Q_WIDTH)


def multiscale_pool(u):
    s = u.shape[1]
    cs = jnp.cumsum(u.astype(jnp.float32), axis=1)
    cs = jnp.concatenate([jnp.zeros_like(cs[:, :1]), cs], axis=1)
    t = jnp.arange(s)[:, None]
    w = jnp.array(POOL_WINDOWS, dtype=jnp.int32)[None, :]
    lo = jnp.clip(t - w // 2, 0, s)
    hi = jnp.clip(t + w - w // 2, 0, s)
    g_idx = jnp.arange(N_POOL_GROUPS)[None, :]
    window_sum = cs[:, hi, g_idx] - cs[:, lo, g_idx]
    count = (hi - lo).astype(jnp.float32)[None, :, :, None]
    return (window_sum / count - u.astype(jnp.float32)).astype(u.dtype)


def token_mixer(n, w_in, q_norm_g, k_norm_g, w_attn_proj, w_pool, pool_scale, w_out):
    b, s, _ = n.shape
    proj = n @ w_in
    o1 = Q_WIDTH
    o2 = o1 + KV_WIDTH
    o3 = o2 + KV_WIDTH
    o4 = o3 + POOL_WIDTH
    q = proj[..., :o1].reshape(b, s, N_Q_HEADS, HEAD_DIM)
    k = proj[..., o1:o2].reshape(b, s, N_KV_HEADS, HEAD_DIM)
    v = proj[..., o2:o3].reshape(b, s, N_KV_HEADS, HEAD_DIM)
    u = proj[..., o3:o4].reshape(b, s, N_POOL_GROUPS, POOL_GROUP_C)
    g_attn = proj[..., o4:o4 + D_MODEL]
    g_pool = proj[..., o4 + D_MODEL:]
    q = axial_rope(rmsnorm(q, q_norm_g), s)
    k = axial_rope(rmsnorm(k, k_norm_g), s)
    attn = blocked_gqa(q, k, v) @ w_attn_proj
    d = multiscale_pool(u)
    pool = jnp.einsum('bsgc,gce->bsge', d, w_pool).reshape(b, s, D_MODEL) * pool_scale
    merged = jax.nn.sigmoid(g_attn) * attn + jax.nn.sigmoid(g_pool) * pool
    return merged @ w_out


def channel_mixer(n, w_up, conv_w, conv_b, w_down):
    up = n @ w_up
    up_p = jnp.pad(up, ((0, 0), (1, 1), (0, 0)))
    c = up_p[:, :-2] * conv_w[0] + up_p[:, 1:-1] * conv_w[1] + up_p[:, 2:] * conv_w[2] + conv_b
    gate, val = c[..., :D_FF], c[..., D_FF:]
    return (jax.nn.silu(gate) * val) @ w_down


def setup_inputs(seed: int = 0) -> dict:
    key = jax.random.key(seed)
    ks = jax.random.split(key, 16)
    f = jnp.float32

    def nrm(k, shape, fan_in):
        return jax.random.normal(k, shape, f) * (fan_in ** -0.5)

    return {
        "x_prompt": jax.random.normal(ks[0], (BATCH, SEQ, D_MODEL), f),
        "x_sample": jax.random.normal(ks[1], (DEC_BATCH, DEC_SEQ, D_MODEL), f),
        "norm_mix_g": 1.0 + 0.02 * jax.random.normal(ks[2], (DEPTH, D_MODEL), f),
        "w_in": nrm(ks[3], (DEPTH, D_MODEL, IN_WIDTH), D_MODEL),
        "q_norm_g": 1.0 + 0.02 * jax.random.normal(ks[4], (DEPTH, HEAD_DIM), f),
        "k_norm_g": 1.0 + 0.02 * jax.random.normal(ks[5], (DEPTH, HEAD_DIM), f),
        "w_attn_proj": nrm(ks[6], (DEPTH, Q_WIDTH, D_MODEL), Q_WIDTH),
        "w_pool": nrm(ks[7], (DEPTH, N_POOL_GROUPS, POOL_GROUP_C, POOL_OUT_C), POOL_GROUP_C),
        "pool_scale": 1.0 + 0.02 * jax.random.normal(ks[8], (DEPTH, D_MODEL), f),
        "w_out": nrm(ks[9], (DEPTH, D_MODEL, D_MODEL), D_MODEL),
        "norm_ffn_g": 1.0 + 0.02 * jax.random.normal(ks[10], (DEPTH, D_MODEL), f),
        "w_up": nrm(ks[11], (DEPTH, D_MODEL, 2 * D_FF), D_MODEL),
        "conv_w": nrm(ks[12], (DEPTH, CONV_W, 2 * D_FF), CONV_W),
        "conv_b": 0.02 * jax.random.normal(ks[13], (DEPTH, 2 * D_FF), f),
        "w_down": nrm(ks[14], (DEPTH, D_FF, D_MODEL), D_FF),
        "norm_final_g": 1.0 + 0.02 * jax.random.normal(ks[15], (D_MODEL,), f),
    }


def reference(x_prompt, x_sample, norm_mix_g, w_in, q_norm_g, k_norm_g, w_attn_proj, w_pool,
              pool_scale, w_out, norm_ffn_g, w_up, conv_w, conv_b, w_down, norm_final_g):
    def trunk(x):
        h = x
        for l in range(DEPTH):
            h = h + token_mixer(rmsnorm(h, norm_mix_g[l]), w_in[l], q_norm_g[l], k_norm_g[l],
                                w_attn_proj[l], w_pool[l], pool_scale[l], w_out[l])
            h = h + channel_mixer(rmsnorm(h, norm_ffn_g[l]), w_up[l], conv_w[l], conv_b[l], w_down[l])
        return rmsnorm(h, norm_final_g)

    y_prompt = trunk(x_prompt)
    y_sample = trunk(x_sample)
    return (y_prompt, y_sample)
```

```python
import numpy as np
from contextlib import ExitStack
import concourse.bass as bass
import concourse.mybir as mybir
from concourse.bass_utils import run_bass_kernel_spmd

F32 = mybir.dt.float32
BF16 = mybir.dt.bfloat16
AF = mybir.ActivationFunctionType
ALU = mybir.AluOpType
AX = mybir.AxisListType

D = 4096
DC = 32
TOK = 512
TC = 514
XC = 530
HALVES = ((0, 258), (258, 514))
UHALVES = ((0, 266), (266, 530))
NQH = 16
NKV = 4
DFF = 11008
FC = 86
EPS = 1e-6
SCALE = 128 ** -0.5
SPLITS = ((0, 22), (22, 22), (44, 21), (65, 21))
NSLOT = 5
PF = 4
NTMP = 8
COL_Q, COL_K, COL_V, COL_U, COL_GA, COL_GP = 0, 2048, 2560, 3072, 5120, 9216
SP_LEN = 2048
SS_LEN = 4096
XPAD = 16

DMA_RING = 8
ENGS = ("pe", "act", "dve", "pool", "sp")


class Op:
    __slots__ = ("eng", "fn", "deps", "is_dma", "signal", "tok", "waits", "n")

    def __init__(self, eng, fn, is_dma):
        self.eng = eng
        self.fn = fn
        self.is_dma = is_dma
        self.deps = []
        self.signal = False
        self.tok = None
        self.waits = None
        self.n = 0


class Prog:
    def __init__(self):
        self.ops = {e: [] for e in ENGS}
        self.res_w = {}
        self.res_r = {}
        self.res_x = {}
        self.dma_n = {e: 0 for e in ENGS}
        self.dma_ring = {e: [None] * DMA_RING for e in ENGS}

    @staticmethod
    def _compact(lst):
        out = []
        seen = set()
        for op in reversed(lst):
            if op.is_dma:
                out.append(op)
            elif op.eng not in seen:
                seen.add(op.eng)
                out.append(op)
        out.reverse()
        return out

    def add(self, eng, fn, reads=(), writes=(), dma=False):
        op = Op(eng, fn, dma)
        deps = {}
        res_w = self.res_w
        res_r = self.res_r
        res_x = self.res_x
        for k in reads:
            for d in res_w.get(k, ()):
                deps[id(d)] = d
            if type(k) is tuple and k[0] == "ps":
                for d in res_x.get(k, ()):
                    if d.eng != eng:
                        deps[id(d)] = d
        for k in writes:
            for d in res_w.get(k, ()):
                deps[id(d)] = d
            for d in res_r.get(k, ()):
                deps[id(d)] = d
            for d in res_x.get(k, ()):
                deps[id(d)] = d
        if dma:
            n = self.dma_n[eng]
            op.n = n
            prev = self.dma_ring[eng][n % DMA_RING]
            if prev is not None:
                deps[id(prev)] = prev
            self.dma_ring[eng][n % DMA_RING] = op
            self.dma_n[eng] = n + 1
        for d in deps.values():
            if d is op:
                continue
            if eng == "pe" and d.eng == "pe" and not d.is_dma and not dma:
                continue
            d.signal = True
            op.deps.append(d)
        for k in reads:
            if type(k) is tuple and k[0] == "ps":
                res_x[k] = self._compact(list(res_x.get(k, ())) + [op])
                continue
            lst = res_r.get(k)
            if lst is None:
                res_r[k] = [op]
            else:
                lst.append(op)
                if len(lst) > 6:
                    res_r[k] = self._compact(lst)
        for k in writes:
            if res_r.get(k) or res_x.get(k):
                res_w[k] = [op]
                res_r[k] = []
                res_x[k] = []
            else:
                lst = res_w.get(k)
                if lst is None:
                    res_w[k] = [op]
                else:
                    lst.append(op)
                    if len(lst) > 6:
                        res_w[k] = self._compact(lst)
        self.ops[eng].append(op)
        return op

    def finalize(self, eng_sems, ring_sems):
        for e in ENGS:
            cnt = 0
            for op in self.ops[e]:
                if op.is_dma:
                    op.signal = True
                    op.tok = (ring_sems[e][op.n % DMA_RING], 16 * (op.n // DMA_RING + 1), 16)
                elif op.signal:
                    cnt += 1
                    op.tok = (eng_sems[e], cnt, 1)
        nw = 0
        for e in ENGS:
            waited = {}
            for op in self.ops[e]:
                need = {}
                for d in op.deps:
                    sem, val, _ = d.tok
                    key = id(sem)
                    if waited.get(key, 0) >= val:
                        continue
                    if key not in need or need[key][1] < val:
                        need[key] = (sem, val)
                op.waits = list(need.values())
                for key, (sem, val) in need.items():
                    waited[key] = val
                nw += len(op.waits)
        return nw

    def emit_engine(self, e, engine):
        for op in self.ops[e]:
            for sem, val in op.waits:
                engine.wait_ge(sem, val)
            if op.fn is None:
                continue
            ins = op.fn(engine)
            if op.signal:
                sem, _, amt = op.tok
                ins.then_inc(sem, amt)

    def stats(self):
        return {e: len(self.ops[e]) for e in ENGS}


class WStream:
    def __init__(self, slots):
        self.slots = slots
        self.reqs = []
        self.mode = "record"
        self.pos = 0
        self.issued = 0
        self.P = None

    def start(self, P, mode):
        self.P = P
        self.mode = mode
        self.pos = 0
        self.issued = 0
        if mode == "record":
            self.reqs = []

    def view(self, i, ks, cw):
        return self.slots[i % NSLOT][:, 0:ks * cw].rearrange("p (k n) -> p k n", n=cw)

    def _issue(self, j):
        w3, k0, ks, c0, cw = self.reqs[j]
        dst = self.view(j, ks, cw)
        src = w3[:, k0:k0 + ks, c0:c0 + cw]
        self.P.add("pool", lambda e: e.dma_start(out=dst, in_=src), writes=[("W", j % NSLOT)], dma=True)

    def get(self, w3, k0, ks, c0, cw):
        i = self.pos
        self.pos += 1
        if self.mode == "record":
            self.reqs.append((w3, k0, ks, c0, cw))
        else:
            last = min(i + PF, len(self.reqs) - 1)
            while self.issued <= last:
                self._issue(self.issued)
                self.issued += 1
        return self.view(i, ks, cw), ("W", i % NSLOT)


def build_nc(n_tiles=6, do_phase0=True):
    nc = bass.Bass("TRN2", target_bir_lowering=False)

    def din(name, shape, dt=F32):
        return nc.dram_tensor(name, list(shape), dt, kind="ExternalInput").ap()

    xp = din("xp", [SP_LEN + 2 * XPAD, D])
    xs = din("xs", [SS_LEN, D])
    xq = din("xq", [1024 + 2 * XPAD, D])
    w_in = din("w_in", [D, 13312])
    w_attn = din("w_attn", [2048, D])
    w_pool = din("w_pool", [2048, 1024])
    w_out = din("w_out", [D, D])
    w_up = din("w_up", [D, 2 * DFF])
    w_down = din("w_down", [DFF, D])
    gmix_d = din("gmixT", [128, DC])
    gffn_d = din("gffnT", [128, DC])
    gfin_d = din("gfinT", [128, DC])
    pscale_d = din("pscaleT", [128, DC])
    qg_d = din("qg", [128, 1])
    kg_d = din("kg", [128, 1])
    cw_d = din("convw", [128, 3 * 2 * FC])
    cb_d = din("convb", [128, 2 * FC])
    ident_d = din("ident", [128, 128])
    rmat_d = din("rmat", [128, 128])
    ropet_c = din("ropet_c", [128, 6 * TC])
    ropet_s = din("ropet_s", [128, 6 * TC])
    ropek_c = din("ropek_c", [128, SS_LEN])
    ropek_s = din("ropek_s", [128, SS_LEN])
    invcnt_d = din("invcnt", [6, 4 * TC])
    maskrow_d = din("maskrow", [6, TC])
    out_p = nc.dram_tensor("out_p", [SP_LEN, D], F32, kind="ExternalOutput").ap()
    out_s = nc.dram_tensor("out_s", [1024, D], F32, kind="ExternalOutput").ap()
    ktp = nc.dram_tensor("ktp", [NKV, 128, SP_LEN], BF16, kind="Internal").ap()
    vp = nc.dram_tensor("vp", [NKV, 128, SP_LEN // 128, 128], BF16, kind="Internal").ap()
    kts = nc.dram_tensor("kts", [NKV, 128, SS_LEN], BF16, kind="Internal").ap()
    vs = nc.dram_tensor("vs", [NKV, 128, SS_LEN // 128, 128], BF16, kind="Internal").ap()

    w_in3 = w_in.rearrange("(k p) n -> p k n", p=128)
    w_attn3 = w_attn.rearrange("(k p) n -> p k n", p=128)
    w_pool3 = w_pool.rearrange("(k p) n -> p k n", p=128)
    w_out3 = w_out.rearrange("(k p) n -> p k n", p=128)
    w_up3 = w_up.rearrange("(k p) n -> p k n", p=128)
    w_down3 = w_down.rearrange("(k p) n -> p k n", p=128)

    with ExitStack() as es:
        def sb(name, shape, dt):
            return es.enter_context(nc.sbuf_tensor(name, list(shape), dt))

        R1 = sb("R1", [128, 32896], BF16)
        R2 = sb("R2", [128, DC * XC], BF16)
        R3 = sb("R3", [128, DC * TC], BF16)
        wslots = [sb(f"wslot{i}", [128, 2048], BF16) for i in range(NSLOT)]
        xb = [sb(f"xb{i}", [128, 1024], F32) for i in range(2)]
        tmp = [sb(f"tmp{i}", [128, XC], F32) for i in range(NTMP)]
        pT = [sb(f"pT{i}", [128, TC], BF16) for i in range(4)]
        cosb = sb("cosb", [128, TC], F32)
        sinb = sb("sinb", [128, TC], F32)
        invc = sb("invc", [128, 4 * TC], F32)
        maskb = sb("maskb", [128, TC], F32)
        rstdb = sb("rstdb", [128, TC], F32)
        gmix = sb("gmix", [128, DC], F32)
        gffn = sb("gffn", [128, DC], F32)
        gfin = sb("gfin", [128, DC], F32)
        pscale = sb("pscale", [128, DC], F32)
        qg = sb("qgb", [128, 1], F32)
        kg = sb("kgb", [128, 1], F32)
        cw = sb("cw", [128, 3 * 2 * FC], F32)
        cb = sb("cb", [128, 2 * FC], F32)
        ident = sb("identb", [128, 128], F32)
        rmat = sb("rmatb", [128, 128], F32)
        ones_f = sb("ones_f", [128, 128], F32)
        ones_b = sb("ones_b", [128, 128], BF16)
        ssb = sb("ssb", [128, 16], F32)
        ps = [es.enter_context(nc.psum_tensor(f"ps{i}", [128, 512], F32)) for i in range(8)]
        eng_sems = {e: es.enter_context(nc.semaphore(f"sem_{e}")) for e in ("pe", "act", "dve", "pool")}
        ring_sems = {e: [es.enter_context(nc.semaphore(f"ring_{e}{i}")) for i in range(DMA_RING)]
                     for e in ("sp", "act", "pool")}

        hT = R1[:, :].bitcast(F32).rearrange("p (c n) -> p c n", n=TC)
        qT = R1[:, 0:NQH * TC].rearrange("p (c n) -> p c n", n=TC)
        dT = R1[:, NQH * TC:2 * NQH * TC].rearrange("p (c n) -> p c n", n=TC)
        KVOFF = 2 * NQH * TC
        wk = R1[:, 0:16384].rearrange("p (k n) -> p k n", n=512)
        wv = R1[:, 16384:32768].rearrange("p (k n) -> p k n", n=512)
        nT = R2[:, :].rearrange("p (c n) -> p c n", n=XC)
        mT = R3[:, :].rearrange("p (c n) -> p c n", n=TC)
        gT = R3[:, 0:22 * TOK].rearrange("p (c n) -> p c n", n=TOK)
        R3f = R3[:, :].bitcast(F32)
        xstg = [R3f[:, 0:4096], R3f[:, 4096:8192]]

        def R1K(lo, hi):
            return [("R1", b) for b in range(lo // 514, (hi - 1) // 514 + 1)]

        def R3K(lo, hi):
            return [("R3", b) for b in range(lo // 514, (hi - 1) // 514 + 1)]

        def hTK(c):
            return R1K(c * 1028, (c + 1) * 1028)

        def qTK(h):
            return R1K(h * TC, (h + 1) * TC)

        def dTK(j):
            return R1K(NQH * TC + j * TC, NQH * TC + (j + 1) * TC)

        def kvK(i):
            return R1K(KVOFF + i * 8192, KVOFF + (i + 1) * 8192)

        def mTK(c):
            return R3K(c * TC, (c + 1) * TC)

        def gTK(j):
            return R3K(j * TOK, (j + 1) * TOK)

        XSTGK = [R3K(0, 8192), R3K(8192, 16384)]
        ALLR1 = R1K(0, 32896)

        ws = WStream(wslots)

        def emit_program(P):
            def MM(out, lhsT, rhs, start, stop, reads, writes):
                P.add("pe", lambda e: e.matmul(out, lhsT, rhs, start=start, stop=stop), reads, writes)

            def TR(out, in_, idn, reads, writes):
                P.add("pe", lambda e: e.transpose(out, in_, idn), reads, writes)

            def ACT(out, in_, func, reads, writes, scale=None, accum_out=None):
                kw = {}
                if scale is not None:
                    kw["scale"] = scale
                if accum_out is not None:
                    kw["accum_out"] = accum_out
                P.add("act", lambda e: e.activation(out=out, in_=in_, func=func, **kw), reads, writes)

            def TT(out, in0, in1, op, reads, writes, eng="dve"):
                P.add(eng, lambda e: e.tensor_tensor(out=out, in0=in0, in1=in1, op=op), reads, writes)

            def TS(out, in0, s1, s2, op0, op1, reads, writes, eng="dve"):
                P.add(eng, lambda e: e.tensor_scalar(out=out, in0=in0, scalar1=s1, scalar2=s2, op0=op0, op1=op1),
                      reads, writes)

            def TSM(out, in0, s1, reads, writes, eng="dve"):
                P.add(eng, lambda e: e.tensor_scalar_mul(out=out, in0=in0, scalar1=s1), reads, writes)

            def STT(out, in0, scalar, in1, op0, op1, reads, writes, eng="dve"):
                P.add(eng, lambda e: e.scalar_tensor_tensor(out=out, in0=in0, scalar=scalar, in1=in1, op0=op0, op1=op1),
                      reads, writes)

            def CP(out, in_, reads, writes, eng="dve"):
                if eng == "act":
                    P.add("act", lambda e: e.activation(out=out, in_=in_, func=AF.Copy), reads, writes)
                else:
                    P.add(eng, lambda e: e.tensor_copy(out, in_), reads, writes)

            def RCP(out, in_, reads, writes):
                P.add("dve", lambda e: e.reciprocal(out=out, in_=in_), reads, writes)

            def DMA(eng, out, in_, reads, writes):
                P.add(eng, lambda e: e.dma_start(out=out, in_=in_), reads, writes, dma=True)

            def PSK(b):
                return ("ps", b)

            def TK(i):
                return ("tmp", i)

            for dst, src, key in ((gmix, gmix_d, "gmix"), (gffn, gffn_d, "gffn"), (gfin, gfin_d, "gfin"),
                                  (pscale, pscale_d, "pscale"), (qg, qg_d, "qg"), (kg, kg_d, "kg"),
                                  (cw, cw_d, "cw"), (cb, cb_d, "cb"), (ident, ident_d, "ident"),
                                  (rmat, rmat_d, "rmat")):
                DMA("sp", dst[:, :], src[:, :], [], [key])
            P.add("dve", lambda e: e.memset(ones_f[:, :], 1.0), [], ["ones_f"])
            P.add("dve", lambda e: e.memset(ones_b[:, :], 1.0), [], ["ones_b"])

            def norm_s1(src, row0, nr, blk):
                buf = xstg[blk % 2]
                bk = XSTGK[blk % 2]
                so = 8 * (blk % 2)
                sk = ("ss", blk % 2)
                DMA("sp", buf[0:nr, :], src[row0:row0 + nr, :], [], bk)
                for qd in range(4):
                    jt = tmp[6 + (qd % 2)]
                    ACT(jt[0:nr, 0:512].bitcast(BF16), buf[0:nr, qd * 1024:(qd + 1) * 1024], AF.Square,
                        bk, [TK(6 + (qd % 2)), sk], accum_out=ssb[0:nr, so + qd:so + qd + 1])
                P.add("dve", lambda e: e.reduce_sum(out=ssb[0:nr, so + 4:so + 5], in_=ssb[0:nr, so:so + 4], axis=AX.X),
                      [sk], [sk])
                TS(ssb[0:nr, so + 5:so + 6], ssb[0:nr, so + 4:so + 5], 1.0 / D, EPS, ALU.mult, ALU.add, [sk], [sk])
                ACT(ssb[0:nr, so + 6:so + 7], ssb[0:nr, so + 5:so + 6], AF.Sqrt, [sk], [sk])
                RCP(ssb[0:nr, so + 7:so + 8], ssb[0:nr, so + 6:so + 7], [sk], [sk])
                rs = ssb[0:nr, so + 7:so + 8]
                ACT(buf[0:nr, 0:2048], buf[0:nr, 0:2048], AF.Copy, [sk] + bk, bk, scale=rs)
                TSM(buf[0:nr, 2048:4096], buf[0:nr, 2048:4096], rs, [sk] + bk, bk)

            def norm_s2(nr, col0, blk):
                buf = xstg[blk % 2]
                bk = XSTGK[blk % 2]
                for g8 in range(8):
                    b = g8
                    for j in range(4):
                        c = g8 * 4 + j
                        TR(ps[b][:, j * 128:j * 128 + nr], buf[0:nr, c * 128:(c + 1) * 128], ident[0:nr, 0:nr],
                           bk + ["ident"], [PSK(b)])
                    ev = "act" if g8 in (1, 4, 6) else "dve"
                    for j in range(4):
                        c = g8 * 4 + j
                        if ev == "act":
                            ACT(nT[:, c, col0:col0 + nr], ps[b][:, j * 128:j * 128 + nr], AF.Copy,
                                [PSK(b), "gmix"], [("R2", c)], scale=gmix[:, c:c + 1])
                        else:
                            TSM(nT[:, c, col0:col0 + nr], ps[b][:, j * 128:j * 128 + nr], gmix[:, c:c + 1],
                                [PSK(b), "gmix"], [("R2", c)])

            def norm_pipeline(blocks, after=None):
                n = len(blocks)
                for i in range(min(2, n)):
                    norm_s1(blocks[i][0], blocks[i][1], blocks[i][2], i)
                for i in range(n):
                    norm_s2(blocks[i][2], blocks[i][3], i)
                    if i + 2 < n:
                        norm_s1(blocks[i + 2][0], blocks[i + 2][1], blocks[i + 2][2], i + 2)
                    if after is not None:
                        after(i)

            def qk_chain_a(src, skeys, w, i_sq, bss):
                ACT(tmp[i_sq][:, 0:w], src, AF.Square, skeys, [TK(i_sq)])
                MM(ps[bss][:, 0:w], ones_f[:, :], tmp[i_sq][:, 0:w], True, True, ["ones_f", TK(i_sq)], [PSK(bss)])

            def qk_chain_b(src, skeys, w, lo, hi, gvec, gkey, i_sq, i_r, qn_ap, qn_keys, bss, brot, o_ap, o_keys):
                TS(tmp[i_r][:, 0:w], ps[bss][:, 0:w], 1.0 / 128, EPS, ALU.mult, ALU.add, [PSK(bss)], [TK(i_r)])
                ACT(tmp[i_r][:, 0:w], tmp[i_r][:, 0:w], AF.Sqrt, [TK(i_r)], [TK(i_r)])
                RCP(tmp[i_r][:, 0:w], tmp[i_r][:, 0:w], [TK(i_r)], [TK(i_r)])
                STT(qn_ap, src, gvec[:, 0:1], tmp[i_r][:, 0:w], ALU.mult, ALU.mult,
                    skeys + [gkey, TK(i_r)], qn_keys)
                MM(ps[brot][:, 0:w], rmat[:, :], qn_ap, True, True, ["rmat"] + qn_keys, [PSK(brot)])
                TT(tmp[i_sq][:, 0:w], qn_ap, cosb[:, lo:hi], ALU.mult, qn_keys + ["cosb"], [TK(i_sq)])
                TT(tmp[i_r][:, 0:w], ps[brot][:, 0:w], sinb[:, lo:hi], ALU.mult, [PSK(brot), "sinb"], [TK(i_r)])
                TT(o_ap, tmp[i_sq][:, 0:w], tmp[i_r][:, 0:w], ALU.add, [TK(i_sq), TK(i_r)], o_keys)

            def phase0(src, row_base, S, kt_scr, v_scr, tagseq, first):
                if first:
                    for i in range(4):
                        P.add("pool", lambda e, i=i: e.dma_start(out=wk[:, 8 * i:8 * i + 8, :],
                                                                 in_=w_in3[:, 8 * i:8 * i + 8, COL_K:COL_K + 512]),
                              [], R1K(i * 4096, (i + 1) * 4096), dma=True)
                    for i in range(4):
                        P.add("pool", lambda e, i=i: e.dma_start(out=wv[:, 8 * i:8 * i + 8, :],
                                                                 in_=w_in3[:, 8 * i:8 * i + 8, COL_V:COL_V + 512]),
                              [], R1K(16384 + i * 4096, 16384 + (i + 1) * 4096), dma=True)
                WKK = R1K(0, 16384)
                WVK = R1K(16384, 32768)
                v_dst = v_scr.rearrange("kv p c d -> p kv c d")
                ngrp = S // 512
                blocks = [(src, row_base + g_ * 512 + rb * 128, 128, rb * 128) for g_ in range(ngrp) for rb in range(4)]

                def kv_group(grp):
                    r0 = grp * 512
                    DMA("sp", cosb[:, 0:512], ropek_c[:, r0:r0 + 512], [], ["cosb"])
                    DMA("sp", sinb[:, 0:512], ropek_s[:, r0:r0 + 512], [], ["sinb"])

                    def kmm(kv):
                        b = kv % 2
                        for k in range(DC):
                            MM(ps[b][:, 0:512], wk[:, k, kv * 128:(kv + 1) * 128], nT[:, k, 0:512], k == 0, k == DC - 1,
                               WKK + [("R2", k)], [PSK(b)])

                    def kchain(kv):
                        b = kv % 2
                        pi = kv % 2
                        t0 = 3 * (kv % 2)
                        qk_chain_a(ps[b][:, 0:512], [PSK(b)], 512, t0, 2 + (kv % 2))
                        qk_chain_b(ps[b][:, 0:512], [PSK(b)], 512, 0, 512, kg, "kg", t0, t0 + 1,
                                   tmp[t0 + 2][:, 0:512], [TK(t0 + 2)], 2 + (kv % 2), 4 + (kv % 2),
                                   pT[pi][:, 0:512], [("pT", pi)])
                        DMA("act", kt_scr[kv, :, r0:r0 + 512], pT[pi][:, 0:512], [("pT", pi)], [("KT", tagseq, kv, grp)])

                    def vmm(rb):
                        b = 6 + (rb % 2)
                        for k in range(DC):
                            MM(ps[b][:, 0:512], nT[:, k, rb * 128:(rb + 1) * 128], wv[:, k, 0:512], k == 0, k == DC - 1,
                               WVK + [("R2", k)], [PSK(b)])
                        pi = 2 + (rb % 2)
                        ACT(pT[pi][:, 0:512], ps[b][:, 0:512], AF.Copy, [PSK(b)], [("pT", pi)])
                        ch = grp * 4 + rb
                        DMA("act", v_dst[:, :, ch, :], pT[pi][:, 0:512].rearrange("p (kv d) -> p kv d", d=128),
                            [("pT", pi)], [("V", tagseq, ch)])

                    kmm(0)
                    kmm(1)
                    kchain(0)
                    kmm(2)
                    kchain(1)
                    kmm(3)
                    kchain(2)
                    vmm(0)
                    kchain(3)
                    vmm(1)
                    vmm(2)
                    vmm(3)

                def after(i):
                    if i % 4 == 3:
                        kv_group(i // 4)
                norm_pipeline(blocks, after)

            def tile(ti):
                if ti < 4:
                    src, base, S, kt_scr, v_scr, tagseq = xp, 512 * ti + XPAD, SP_LEN, ktp, vp, "p"
                    dst, orow = out_p, 512 * ti
                else:
                    src, base, S, kt_scr, v_scr, tagseq = xq, 512 * (ti - 4) + XPAD, SS_LEN, kts, vs, "s"
                    dst, orow = out_s, 512 * (ti - 4)
                NCH = S // 128
                DMA("sp", cosb[:, :], ropet_c[:, ti * TC:(ti + 1) * TC], [], ["cosb"])
                DMA("sp", sinb[:, :], ropet_s[:, ti * TC:(ti + 1) * TC], [], ["sinb"])
                DMA("sp", invc[:, :], invcnt_d[ti:ti + 1, :].to_broadcast([128, 4 * TC]), [], ["invc"])
                DMA("sp", maskb[:, :], maskrow_d[ti:ti + 1, :].to_broadcast([128, TC]), [], ["maskb"])

                norm_pipeline([(src, base - 9 + rb * 128, 128 if rb < 4 else XC - 512, rb * 128) for rb in range(5)])

                QB = (0, 1, 2, 3)

                def q_mm(hp):
                    for sl in range(4):
                        wv_, wkey = ws.get(w_in3, sl * 8, 8, COL_Q + hp * 256, 256)
                        for kk in range(8):
                            k = sl * 8 + kk
                            for hh in range(2):
                                for hf, (lo, hi) in enumerate(HALVES):
                                    b = QB[hh * 2 + hf]
                                    MM(ps[b][:, 0:hi - lo], wv_[:, kk, hh * 128:(hh + 1) * 128], nT[:, k, 8 + lo:8 + hi],
                                       k == 0, k == DC - 1, [wkey, ("R2", k)], [PSK(b)])

                def q_copy(hp):
                    for hh in range(2):
                        it = 4 + 2 * (hp % 2) + hh
                        for hf, (lo, hi) in enumerate(HALVES):
                            b = QB[hh * 2 + hf]
                            ACT(tmp[it][:, lo:hi], ps[b][:, 0:hi - lo], AF.Copy, [PSK(b)], [TK(it)])

                def q_chain(hp):
                    for hh in range(2):
                        h = hp * 2 + hh
                        it = 4 + 2 * (hp % 2) + hh
                        for hf, (lo, hi) in enumerate(HALVES):
                            w = hi - lo
                            qk_chain_a(tmp[it][:, lo:hi], [TK(it)], w, 2 * hf, 4 + 2 * hf)
                        for hf, (lo, hi) in enumerate(HALVES):
                            w = hi - lo
                            qk_chain_b(tmp[it][:, lo:hi], [TK(it)], w, lo, hi, qg, "qg", 2 * hf, 2 * hf + 1,
                                       tmp[it][:, lo:hi], [TK(it)], 4 + 2 * hf, 5 + 2 * hf,
                                       qT[:, h, lo:hi], qTK(h))

                q_mm(0)
                q_copy(0)
                for hp in range(NQH // 2):
                    if hp + 1 < NQH // 2:
                        q_mm(hp + 1)
                    q_chain(hp)
                    if hp + 1 < NQH // 2:
                        q_copy(hp + 1)

                for up in range(8):
                    bset = (0, 1, 2, 3) if up % 2 == 0 else (4, 5, 6, 7)
                    for sl in range(4):
                        wv_, wkey = ws.get(w_in3, sl * 8, 8, COL_U + up * 256, 256)
                        for kk in range(8):
                            k = sl * 8 + kk
                            for cc in range(2):
                                for hf, (lo, hi) in enumerate(UHALVES):
                                    b = bset[cc * 2 + hf]
                                    MM(ps[b][:, 0:hi - lo], wv_[:, kk, cc * 128:(cc + 1) * 128], nT[:, k, lo:hi],
                                       k == 0, k == DC - 1, [wkey, ("R2", k)], [PSK(b)])
                    for cc in range(2):
                        j = up * 2 + cc
                        g = j // 4
                        t0 = 4 * (j % 2)
                        iu, iw, ia, ib = t0, t0 + 1, t0 + 2, t0 + 3
                        uS, W_, sA, sB = tmp[iu], tmp[iw], tmp[ia], tmp[ib]
                        for hf, (lo, hi) in enumerate(UHALVES):
                            b = bset[cc * 2 + hf]
                            ACT(uS[:, lo:hi], ps[b][:, 0:hi - lo], AF.Copy, [PSK(b)], [TK(iu)])
                        if g == 0:
                            TT(W_[:, 8:522], uS[:, 7:521], uS[:, 8:522], ALU.add, [TK(iu)], [TK(iw)])
                        else:
                            TT(sA[:, 1:529], uS[:, 0:528], uS[:, 1:529], ALU.add, [TK(iu)], [TK(ia)])
                            if g == 1:
                                TT(W_[:, 8:522], sA[:, 7:521], sA[:, 9:523], ALU.add, [TK(ia)], [TK(iw)])
                            else:
                                TT(sB[:, 2:528], sA[:, 1:527], sA[:, 3:529], ALU.add, [TK(ia)], [TK(ib)])
                                if g == 2:
                                    TT(W_[:, 8:522], sB[:, 6:520], sB[:, 10:524], ALU.add, [TK(ib)], [TK(iw)])
                                else:
                                    TT(sA[:, 4:526], sB[:, 2:524], sB[:, 6:528], ALU.add, [TK(ib)], [TK(ia)])
                                    TT(W_[:, 8:522], sA[:, 4:518], sA[:, 12:526], ALU.add, [TK(ia)], [TK(iw)])
                        TT(W_[:, 8:522], W_[:, 8:522], invc[:, g * TC:(g + 1) * TC], ALU.mult, [TK(iw), "invc"], [TK(iw)])
                        TT(dT[:, j, :], W_[:, 8:522], uS[:, 8:522], ALU.subtract, [TK(iw), TK(iu)], dTK(j))

                for kv in range(NKV):
                    kb = kv % 2
                    koff = KVOFF + kb * 8192
                    Kb = R1[:, koff:koff + S]
                    Vb = R1[:, koff + 4096:koff + 4096 + S].rearrange("p (c d) -> p c d", d=128)
                    KK = R1K(koff, koff + 4096)
                    VK = R1K(koff + 4096, koff + 8192)
                    DMA("sp", Kb, kt_scr[kv, :, :], [("KT", tagseq, kv, g_) for g_ in range(S // 512)], KK)
                    DMA("sp", Vb, v_scr[kv, :, :, :], [("V", tagseq, c_) for c_ in range(NCH)], VK)
                    for hh in range(4):
                        h = kv * 4 + hh

                        def QK(c):
                            for hf, (lo, hi) in enumerate(HALVES):
                                b = (c % 2) * 2 + hf
                                MM(ps[b][:, 0:hi - lo], Kb[:, c * 128:(c + 1) * 128], qT[:, h, lo:hi], True, True,
                                   KK + qTK(h), [PSK(b)])
                            for hf, (lo, hi) in enumerate(HALVES):
                                b = (c % 2) * 2 + hf
                                ACT(pT[c % 4][:, lo:hi], ps[b][:, 0:hi - lo], AF.Exp, [PSK(b)], [("pT", c % 4)], scale=SCALE)

                        def PV(c):
                            for hf, (lo, hi) in enumerate(HALVES):
                                MM(ps[4 + hf][:, 0:hi - lo], Vb[:, c, :], pT[c % 4][:, lo:hi], c == 0, c == NCH - 1,
                                   VK + [("pT", c % 4)], [PSK(4 + hf)])
                            for hf, (lo, hi) in enumerate(HALVES):
                                MM(ps[6 + hf][:, 0:hi - lo], ones_b[:, :], pT[c % 4][:, lo:hi], c == 0, c == NCH - 1,
                                   ["ones_b", ("pT", c % 4)], [PSK(6 + hf)])

                        for c in range(NCH + 1):
                            if c < NCH:
                                QK(c)
                            if c >= 1:
                                PV(c - 1)
                        ir = h % 2
                        for hf, (lo, hi) in enumerate(HALVES):
                            RCP(tmp[ir][:, lo:hi], ps[6 + hf][:, 0:hi - lo], [PSK(6 + hf)], [TK(ir)])
                            TT(qT[:, h, lo:hi], ps[4 + hf][:, 0:hi - lo], tmp[ir][:, lo:hi], ALU.mult,
                               [PSK(4 + hf), TK(ir)], qTK(h))

                for cp in range(DC // 2):
                    c0 = cp * 2
                    A = (0, 1, 2, 3)
                    B = (4, 5, 6, 7)
                    tb = 4 * (cp % 2)

                    def proj(w3, col0, nk, bset, act_of_k, key_of_k):
                        nsl = (nk + 7) // 8
                        for sl in range(nsl):
                            ks = min(8, nk - sl * 8)
                            wv_, wkey = ws.get(w3[0], w3[1] + sl * 8, ks, col0, 256)
                            for kk in range(ks):
                                k = sl * 8 + kk
                                for cc in range(2):
                                    for hf, (lo, hi) in enumerate(HALVES):
                                        b = bset[cc * 2 + hf]
                                        MM(ps[b][:, 0:hi - lo], wv_[:, kk, cc * 128:(cc + 1) * 128], act_of_k(k, lo, hi),
                                           k == 0, k == nk - 1, [wkey] + key_of_k(k), [PSK(b)])

                    n_of = lambda k, lo, hi: nT[:, k, 8 + lo:8 + hi]
                    n_key = lambda k: [("R2", k)]
                    proj((w_in3, 0), COL_GA + c0 * 128, DC, A, n_of, n_key)
                    proj((w_attn3, 0), c0 * 128, NQH, B, lambda k, lo, hi: qT[:, k, lo:hi], lambda k: qTK(k))
                    for cc in range(2):
                        m1 = tmp[tb + cc]
                        for hf, (lo, hi) in enumerate(HALVES):
                            ACT(m1[:, lo:hi], ps[A[cc * 2 + hf]][:, 0:hi - lo], AF.Sigmoid, [PSK(A[cc * 2 + hf])], [TK(tb + cc)])
                        for hf, (lo, hi) in enumerate(HALVES):
                            TT(m1[:, lo:hi], m1[:, lo:hi], ps[B[cc * 2 + hf]][:, 0:hi - lo], ALU.mult,
                               [TK(tb + cc), PSK(B[cc * 2 + hf])], [TK(tb + cc)])
                    g = c0 // 8
                    proj((w_in3, 0), COL_GP + c0 * 128, DC, A, n_of, n_key)
                    proj((w_pool3, g * 4), (c0 % 8) * 128, 4, B, lambda k, lo, hi: dT[:, g * 4 + k, lo:hi],
                         lambda k: dTK(g * 4 + k))
                    for cc in range(2):
                        c = c0 + cc
                        m1 = tmp[tb + cc]
                        m2 = tmp[tb + 2 + cc]
                        for hf, (lo, hi) in enumerate(HALVES):
                            ACT(m2[:, lo:hi], ps[A[cc * 2 + hf]][:, 0:hi - lo], AF.Sigmoid, [PSK(A[cc * 2 + hf])], [TK(tb + 2 + cc)])
                        for hf, (lo, hi) in enumerate(HALVES):
                            STT(m2[:, lo:hi], ps[B[cc * 2 + hf]][:, 0:hi - lo], pscale[:, c:c + 1], m2[:, lo:hi],
                                ALU.mult, ALU.mult, [PSK(B[cc * 2 + hf]), "pscale", TK(tb + 2 + cc)], [TK(tb + 2 + cc)])
                        TT(mT[:, c, :], m1[:, 0:TC], m2[:, 0:TC], ALU.add, [TK(tb + cc), TK(tb + 2 + cc)], mTK(c))

                blk = 0
                for rb in range(5):
                    nr = 128 if rb < 4 else TC - 512
                    for fq in range(4):
                        xbuf = xb[blk % 2]
                        xk = ("xb", blk % 2)
                        DMA("sp", xbuf[0:nr, :], src[base - 1 + rb * 128:base - 1 + rb * 128 + nr, fq * 1024:(fq + 1) * 1024],
                            [], [xk])
                        for g2 in range(2):
                            b = (blk * 2 + g2) % 8
                            for j in range(4):
                                TR(ps[b][:, j * 128:j * 128 + nr], xbuf[0:nr, (g2 * 4 + j) * 128:(g2 * 4 + j + 1) * 128],
                                   ident[0:nr, 0:nr], [xk, "ident"], [PSK(b)])
                            cbase = fq * 8 + g2 * 4
                            keys = []
                            for j in range(4):
                                keys += hTK(cbase + j)
                            wkeys = keys if blk > 0 else ALLR1
                            src_ps = ps[b][:, :].rearrange("p (j n) -> p j n", n=128)[:, :, 0:nr]
                            dst_h = hT[:, cbase:cbase + 4, rb * 128:rb * 128 + nr]
                            if g2 == 0:
                                P.add("act", lambda e, d_=dst_h, s_=src_ps: e.activation(out=d_, in_=s_, func=AF.Copy),
                                      [PSK(b)], wkeys)
                            else:
                                CP(dst_h, src_ps, [PSK(b)], wkeys)
                        blk += 1

                for cp in range(DC // 2):
                    c0 = cp * 2
                    bset = (0, 1, 2, 3) if cp % 2 == 0 else (4, 5, 6, 7)
                    for sl in range(4):
                        wv_, wkey = ws.get(w_out3, sl * 8, 8, c0 * 128, 256)
                        for kk in range(8):
                            k = sl * 8 + kk
                            for cc in range(2):
                                for hf, (lo, hi) in enumerate(HALVES):
                                    b = bset[cc * 2 + hf]
                                    MM(ps[b][:, 0:hi - lo], wv_[:, kk, cc * 128:(cc + 1) * 128], mT[:, k, lo:hi],
                                       k == 0, k == DC - 1, [wkey] + mTK(k), [PSK(b)])
                    for cc in range(2):
                        c = c0 + cc
                        for hf, (lo, hi) in enumerate(HALVES):
                            b = bset[cc * 2 + hf]
                            TT(hT[:, c, lo:hi], hT[:, c, lo:hi], ps[b][:, 0:hi - lo], ALU.add, [PSK(b)] + hTK(c), hTK(c))

                def fm_rstd(apply_mask):
                    for c in range(DC):
                        it = c % 4
                        ACT(tmp[it][:, 0:TC], hT[:, c, :], AF.Square, hTK(c), [TK(it)])
                        for hf, (lo, hi) in enumerate(HALVES):
                            MM(ps[hf][:, 0:hi - lo], ones_f[:, :], tmp[it][:, lo:hi], c == 0, c == DC - 1,
                               ["ones_f", TK(it)], [PSK(hf)])
                    for hf, (lo, hi) in enumerate(HALVES):
                        TS(rstdb[:, lo:hi], ps[hf][:, 0:hi - lo], 1.0 / D, EPS, ALU.mult, ALU.add, [PSK(hf)], ["rstdb"])
                    ACT(rstdb[:, :], rstdb[:, :], AF.Sqrt, ["rstdb"], ["rstdb"])
                    RCP(rstdb[:, :], rstdb[:, :], ["rstdb"], ["rstdb"])
                    if apply_mask:
                        TT(rstdb[:, :], rstdb[:, :], maskb[:, :], ALU.mult, ["rstdb", "maskb"], ["rstdb"])

                fm_rstd(True)
                for c in range(DC):
                    STT(nT[:, c, 0:TC], hT[:, c, :], gffn[:, c:c + 1], rstdb[:, :], ALU.mult, ALU.mult,
                        hTK(c) + ["gffn", "rstdb"], [("R2", c)], eng="dve")

                fchunk = 0
                for (f0, nf) in SPLITS:
                    for jj in range(nf):
                        f = f0 + jj
                        bset = (0, 1, 2, 3) if fchunk % 2 == 0 else (4, 5, 6, 7)
                        t0 = 4 * (fchunk % 2)
                        fchunk += 1
                        for sl in range(4):
                            wv_, wkey = ws.get(w_up3, sl * 8, 8, f * 256, 256)
                            for kk in range(8):
                                k = sl * 8 + kk
                                for gvi in range(2):
                                    for hf, (lo, hi) in enumerate(HALVES):
                                        b = bset[gvi * 2 + hf]
                                        MM(ps[b][:, 0:hi - lo], wv_[:, kk, gvi * 128:(gvi + 1) * 128], nT[:, k, lo:hi],
                                           k == 0, k == DC - 1, [wkey, ("R2", k)], [PSK(b)])
                        iug, iuv, iag, iav = t0, t0 + 1, t0 + 2, t0 + 3
                        for gvi, (iu, ia) in enumerate(((iug, iag), (iuv, iav))):
                            fi = gvi * FC + f
                            for hf, (lo, hi) in enumerate(HALVES):
                                b = bset[gvi * 2 + hf]
                                if gvi == 0:
                                    ACT(tmp[iu][:, lo:hi], ps[b][:, 0:hi - lo], AF.Copy, [PSK(b)], [TK(iu)])
                                else:
                                    CP(tmp[iu][:, lo:hi], ps[b][:, 0:hi - lo], [PSK(b)], [TK(iu)], eng="act" if hf == 0 else "dve")
                            TS(tmp[ia][:, 0:TOK], tmp[iu][:, 0:TOK], cw[:, fi:fi + 1], cb[:, fi:fi + 1], ALU.mult, ALU.add,
                               [TK(iu), "cw", "cb"], [TK(ia)])
                            STT(tmp[ia][:, 0:TOK], tmp[iu][:, 1:TOK + 1], cw[:, 2 * FC + fi:2 * FC + fi + 1], tmp[ia][:, 0:TOK],
                                ALU.mult, ALU.add, [TK(iu), "cw", TK(ia)], [TK(ia)])
                            STT(tmp[ia][:, 0:TOK], tmp[iu][:, 2:TOK + 2], cw[:, 4 * FC + fi:4 * FC + fi + 1], tmp[ia][:, 0:TOK],
                                ALU.mult, ALU.add, [TK(iu), "cw", TK(ia)], [TK(ia)])
                        ACT(tmp[iag][:, 0:TOK], tmp[iag][:, 0:TOK], AF.Silu, [TK(iag)], [TK(iag)])
                        TT(gT[:, jj, :], tmp[iag][:, 0:TOK], tmp[iav][:, 0:TOK], ALU.mult, [TK(iag), TK(iav)], gTK(jj))
                    for cp in range(DC // 2):
                        c0 = cp * 2
                        bset = ((0, 1), (2, 3), (4, 5), (6, 7))[cp % 4]
                        nsl = (nf + 7) // 8
                        for sl in range(nsl):
                            ks = min(8, nf - sl * 8)
                            wv_, wkey = ws.get(w_down3, f0 + sl * 8, ks, c0 * 128, 256)
                            for kk in range(ks):
                                k = sl * 8 + kk
                                for cc in range(2):
                                    b = bset[cc]
                                    MM(ps[b][:, 0:TOK], wv_[:, kk, cc * 128:(cc + 1) * 128], gT[:, k, :],
                                       k == 0, k == nf - 1, [wkey] + gTK(k), [PSK(b)])
                        for cc in range(2):
                            c = c0 + cc
                            b = bset[cc]
                            TT(hT[:, c, 1:TOK + 1], hT[:, c, 1:TOK + 1], ps[b][:, 0:TOK], ALU.add, [PSK(b)] + hTK(c), hTK(c))

                fm_rstd(False)
                for c in range(DC):
                    STT(hT[:, c, :], hT[:, c, :], gfin[:, c:c + 1], rstdb[:, :], ALU.mult, ALU.mult,
                        hTK(c) + ["gfin", "rstdb"], hTK(c), eng="dve")
                for rb in range(4):
                    obuf = xstg[rb % 2]
                    ok = XSTGK[rb % 2]
                    for g8 in range(8):
                        b = (rb * 8 + g8) % 8
                        rkeys = []
                        for j in range(4):
                            c = g8 * 4 + j
                            rkeys += hTK(c)
                            TR(ps[b][:, j * 128:(j + 1) * 128], hT[:, c, 1 + rb * 128:1 + (rb + 1) * 128], ident[:, :],
                               hTK(c) + ["ident"], [PSK(b)])
                        if g8 % 2 == 0:
                            ACT(obuf[:, g8 * 512:(g8 + 1) * 512], ps[b][:, 0:512], AF.Copy, [PSK(b)], ok)
                        else:
                            CP(obuf[:, g8 * 512:(g8 + 1) * 512], ps[b][:, 0:512], [PSK(b)], ok)
                    DMA("act", dst[orow + rb * 128:orow + (rb + 1) * 128, :], obuf[:, :], ok, [("out", ti, rb)])

            if do_phase0:
                phase0(xp, XPAD, SP_LEN, ktp, vp, "p", True)
                phase0(xs, 0, SS_LEN, kts, vs, "s", False)
            for ti in range(n_tiles):
                tile(ti)
            P.add("sp", None, reads=[("out", ti, rb) for ti in range(n_tiles) for rb in range(4)])

        P0 = Prog()
        ws.start(P0, "record")
        emit_program(P0)
        P = Prog()
        ws.start(P, "run")
        emit_program(P)
        nw = P.finalize(eng_sems, ring_sems)
        print("[kernel] ops per engine:", P.stats(), "waits:", nw, "weight slabs:", len(ws.reqs), flush=True)

        with nc.Block() as block:
            @block.sync
            def _(e):
                P.emit_engine("sp", e)

            @block.scalar
            def _(e):
                P.emit_engine("act", e)

            @block.vector
            def _(e):
                P.emit_engine("dve", e)

            @block.tensor
            def _(e):
                P.emit_engine("pe", e)

            @block.gpsimd
            def _(e):
                P.emit_engine("pool", e)
    return nc


def _rope_table(pos):
    pos = np.asarray(pos)
    inv_freq = (np.float32(10000.0) ** (-(np.arange(0, 64, 2, dtype=np.float32) / np.float32(64)))).astype(np.float32)
    row = (pos // 64).astype(np.float32)
    col = (pos % 64).astype(np.float32)
    ang_r = row[:, None] * inv_freq[None, :]
    ang_c = col[:, None] * inv_freq[None, :]
    ang = np.concatenate([ang_r, ang_r, ang_c, ang_c], axis=-1).astype(np.float32)
    return np.ascontiguousarray(np.cos(ang).T.astype(np.float32)), np.ascontiguousarray(np.sin(ang).T.astype(np.float32))


def _tile_tables(q):
    cos_l, sin_l, inv_l, mask_l = [], [], [], []
    for ti in range(6):
        if ti < 4:
            t0, S = 512 * ti, SP_LEN
        else:
            t0, S = 1024 * q + 512 * (ti - 4), SS_LEN
        t = np.arange(t0 - 1, t0 + 513)
        valid = (t >= 0) & (t < S)
        c, s = _rope_table(np.clip(t, 0, S - 1))
        cos_l.append(c)
        sin_l.append(s)
        inv = np.ones((4, TC), np.float32)
        for g, w in enumerate((2, 4, 8, 16)):
            lo = np.clip(t - w // 2, 0, S)
            hi = np.clip(t + w - w // 2, 0, S)
            cnt = np.where(valid, hi - lo, 1).astype(np.float32)
            inv[g] = (np.float32(1.0) / cnt).astype(np.float32)
        inv_l.append(inv.reshape(-1))
        m = np.ones(TC, np.float32)
        m[0] = 1.0 if valid[0] else 0.0
        m[-1] = 1.0 if valid[-1] else 0.0
        mask_l.append(m)
    return (np.ascontiguousarray(np.concatenate(cos_l, axis=1)), np.ascontiguousarray(np.concatenate(sin_l, axis=1)),
            np.ascontiguousarray(np.stack(inv_l)), np.ascontiguousarray(np.stack(mask_l)))


_NC_CACHE = {}


def _fm(v):
    return np.ascontiguousarray(np.asarray(v, np.float32).reshape(-1, 128).T)


def kernel(x_prompt, x_sample, norm_mix_g, w_in, q_norm_g, k_norm_g, w_attn_proj, w_pool,
           pool_scale, w_out, norm_ffn_g, w_up, conv_w, conv_b, w_down, norm_final_g):
    f32 = np.float32
    x_prompt = np.asarray(x_prompt, f32)
    x_sample = np.asarray(x_sample, f32)
    w_in2 = np.ascontiguousarray(np.asarray(w_in, f32)[0])
    w_attn2 = np.ascontiguousarray(np.asarray(w_attn_proj, f32)[0])
    w_pool2 = np.ascontiguousarray(np.asarray(w_pool, f32)[0].reshape(2048, 1024))
    w_out2 = np.ascontiguousarray(np.asarray(w_out, f32)[0])
    w_up2 = np.ascontiguousarray(np.asarray(w_up, f32)[0].reshape(D, 2, FC, 128).transpose(0, 2, 1, 3).reshape(D, 2 * DFF))
    w_down2 = np.ascontiguousarray(np.asarray(w_down, f32)[0])
    cwv = np.asarray(conv_w, f32)[0]
    cw_fm = np.ascontiguousarray(cwv.reshape(3, 2 * FC, 128).transpose(2, 0, 1).reshape(128, 3 * 2 * FC))
    cb_fm = np.ascontiguousarray(np.asarray(conv_b, f32)[0].reshape(2 * FC, 128).T)
    ident = np.eye(128, dtype=f32)
    rmat = np.zeros((128, 128), f32)
    for m in range(128):
        if (m % 64) < 32:
            rmat[m + 32, m] = -1.0
        else:
            rmat[m - 32, m] = 1.0
    ropek_c, ropek_s = _rope_table(np.arange(SS_LEN))
    shared = {
        "w_in": w_in2, "w_attn": w_attn2, "w_pool": w_pool2, "w_out": w_out2, "w_up": w_up2, "w_down": w_down2,
        "gmixT": _fm(np.asarray(norm_mix_g)[0]), "gffnT": _fm(np.asarray(norm_ffn_g)[0]), "gfinT": _fm(np.asarray(norm_final_g)),
        "pscaleT": _fm(np.asarray(pool_scale)[0]),
        "qg": np.ascontiguousarray(np.asarray(q_norm_g, f32)[0].reshape(128, 1)),
        "kg": np.ascontiguousarray(np.asarray(k_norm_g, f32)[0].reshape(128, 1)),
        "convw": cw_fm, "convb": cb_fm, "ident": ident, "rmat": rmat,
        "ropek_c": ropek_c, "ropek_s": ropek_s,
    }
    in_maps = []
    for c in range(8):
        q = c % 4
        sq = c // 4
        xp_pad = np.zeros((SP_LEN + 2 * XPAD, D), f32)
        xp_pad[XPAD:XPAD + SP_LEN] = x_prompt[c]
        xq_pad = np.zeros((1024 + 2 * XPAD, D), f32)
        lo = 1024 * q - XPAD
        hi = 1024 * q + 1024 + XPAD
        slo, shi = max(lo, 0), min(hi, SS_LEN)
        xq_pad[slo - lo:shi - lo] = x_sample[sq, slo:shi]
        tc_, ts_, inv_, mask_ = _tile_tables(q)
        m = dict(shared)
        m.update({"xp": xp_pad, "xs": np.ascontiguousarray(x_sample[sq]), "xq": xq_pad,
                  "ropet_c": tc_, "ropet_s": ts_, "invcnt": inv_, "maskrow": mask_})
        in_maps.append(m)
    if "nc" not in _NC_CACHE:
        _NC_CACHE["nc"] = build_nc()
    nc = _NC_CACHE["nc"]
    res = run_bass_kernel_spmd(nc, in_maps, core_ids=list(range(8)))
    y_prompt = np.empty((8, SP_LEN, D), f32)
    y_sample = np.empty((2, SS_LEN, D), f32)
    for c in range(8):
        r = res.results[c]
        y_prompt[c] = np.asarray(r["out_p"], f32)
        y_sample[c // 4, 1024 * (c % 4):1024 * (c % 4 + 1)] = np.asarray(r["out_s"], f32)
    return (y_prompt, y_sample)
```
